# Optimizing a Trainium2 kernel written in Bass

```python
import numpy as np
import jax, jax.numpy as jnp
from jax import lax

D_MODEL = 1024
BATCH = 1
SEQ = 16384
DEPTH = 2

HEAD_DIM = 64
N_A_LAYERS = DEPTH // 2
N_B_LAYERS = DEPTH - N_A_LAYERS
MEM_LEN = 256
MEM_HEADS = 4
MEM_W = MEM_HEADS * HEAD_DIM
CONV_W = D_MODEL - MEM_W
CONV_K = 31
NSA_HEADS = (D_MODEL - MEM_W) // HEAD_DIM
NSA_W = NSA_HEADS * HEAD_DIM
NSA_KV_GROUPS = 2
HEADS_PER_GROUP = NSA_HEADS // NSA_KV_GROUPS
CMP_L = 32
CMP_STRIDE = 16
CMP_HID = 256
SEL_L = 64
N_SEL = 16
WIN = 512
Q_BLOCK = 128
D_FF = 4 * D_MODEL
KV_W = 6 * NSA_KV_GROUPS * HEAD_DIM
A_IN_W = 2 * CONV_W + MEM_W
B_IN_W = NSA_W + MEM_W + 3 * NSA_HEADS

kernel_name = 'yoco_conformer_nsa_hybrid'


def rms_norm(x, g, eps=1e-6):
    xf = x.astype(jnp.float32)
    y = xf * lax.rsqrt(jnp.mean(xf * xf, axis=-1, keepdims=True) + eps)
    return (y * g.astype(jnp.float32)).astype(x.dtype)


def layer_norm(x, g, b, eps=1e-5):
    xf = x.astype(jnp.float32)
    mu = jnp.mean(xf, axis=-1, keepdims=True)
    var = jnp.mean(jnp.square(xf - mu), axis=-1, keepdims=True)
    y = (xf - mu) * lax.rsqrt(var + eps)
    return (y * g.astype(jnp.float32) + b.astype(jnp.float32)).astype(x.dtype)


def masked_softmax(s, mask):
    s = jnp.where(mask, s.astype(jnp.float32), -jnp.inf)
    m = jnp.max(s, axis=-1, keepdims=True)
    m = jnp.where(jnp.isfinite(m), m, 0.0)
    e = jnp.exp(s - m)
    return e / jnp.maximum(jnp.sum(e, axis=-1, keepdims=True), 1e-30)


def selection_map(n_cmp, n_sb):
    cs = np.arange(n_cmp)[:, None] * CMP_STRIDE
    ss = np.arange(n_sb)[None, :] * SEL_L
    ov = np.minimum(cs + CMP_L, ss + SEL_L) - np.maximum(cs, ss)
    return (np.clip(ov, 0, None) / CMP_STRIDE).astype(np.float32)


def squared_relu_mlp(h, w_in, w_out):
    return jnp.square(jax.nn.relu(h @ w_in)) @ w_out


def memory_attention(q, mk, mv):
    B, S = q.shape[:2]
    s = jnp.einsum('bshd,bmhd->bhsm', q, mk) * HEAD_DIM ** -0.5
    p = jax.nn.softmax(s.astype(jnp.float32), axis=-1)
    return jnp.einsum('bhsm,bmhd->bshd', p.astype(mv.dtype), mv).reshape(B, S, MEM_W)


def conformer_conv(u, dw, dw_b, ln_g, ln_b):
    a, gate = jnp.split(u, 2, axis=-1)
    v = a * jax.nn.sigmoid(gate)
    v = lax.conv_general_dilated(v, dw[:, None, :], window_strides=(1,),
                                 padding=[(CONV_K - 1, 0)],
                                 dimension_numbers=('NWC', 'WIO', 'NWC'),
                                 feature_group_count=CONV_W) + dw_b
    return jax.nn.silu(layer_norm(v, ln_g, ln_b))


def shared_nsa_kv(h, kv_norm_g, w_kv, k_norm_g, pe_k, pe_v, w1_k, w2_k, w1_v, w2_v):
    B, S = h.shape[:2]
    G, HD = NSA_KV_GROUPS, HEAD_DIM
    kv = (rms_norm(h, kv_norm_g) @ w_kv).reshape(B, S, 6, G, HD)
    k_c, v_c, k_s, v_s, k_w, v_w = (kv[:, :, i] for i in range(6))
    n_cmp = (S - CMP_L) // CMP_STRIDE + 1
    idx = np.arange(n_cmp)[:, None] * CMP_STRIDE + np.arange(CMP_L)[None, :]

    def compress(t, pe, w1, w2):
        blk = t[:, idx] + pe[:, None, :]
        blk = blk.transpose(0, 1, 3, 2, 4).reshape(B, n_cmp, G, CMP_L * HD)
        return jax.nn.gelu(blk @ w1) @ w2

    k_cmp = rms_norm(compress(k_c, pe_k, w1_k, w2_k), k_norm_g[0])
    v_cmp = compress(v_c, pe_v, w1_v, w2_v)
    n_sb = S // SEL_L
    k_blk = rms_norm(k_s, k_norm_g[1]).reshape(B, n_sb, SEL_L, G, HD).transpose(0, 3, 1, 2, 4)
    v_blk = v_s.reshape(B, n_sb, SEL_L, G, HD).transpose(0, 3, 1, 2, 4)
    pad = ((0, 0), (WIN, 0), (0, 0), (0, 0))
    k_win = jnp.pad(rms_norm(k_w, k_norm_g[2]), pad)
    v_win = jnp.pad(v_w, pad)
    return k_cmp, v_cmp, k_blk, v_blk, k_win, v_win


def nsa_attention(q, gates, k_cmp, v_cmp, k_blk, v_blk, k_win, v_win):
    B, S = q.shape[:2]
    G, HG, HD = NSA_KV_GROUPS, HEADS_PER_GROUP, HEAD_DIM
    n_qb = S // Q_BLOCK
    n_cmp = k_cmp.shape[1]
    n_sb = k_blk.shape[2]
    n_sel = min(N_SEL, n_sb)
    scale = HD ** -0.5
    cmp_end = jnp.asarray(np.arange(n_cmp) * CMP_STRIDE + CMP_L - 1)
    sel_map = jnp.asarray(selection_map(n_cmp, n_sb))
    blk_ids = jnp.arange(n_sb)
    b_ix = jnp.arange(B)[:, None, None, None]
    g_ix = jnp.arange(G)[None, None, :, None]
    qs = q.reshape(B, n_qb, Q_BLOCK, G, HG, HD).transpose(1, 0, 2, 3, 4, 5)
    gs = gates.reshape(B, n_qb, Q_BLOCK, G, HG, 3).transpose(1, 0, 2, 3, 4, 5)

    def block(args):
        c, qb, gb = args
        t = c * Q_BLOCK + jnp.arange(Q_BLOCK)
        s = jnp.einsum('bqghd,bngd->bqghn', qb, k_cmp) * scale
        p_cmp = masked_softmax(s, (cmp_end[None, :] <= t[:, None])[None, :, None, None, :])
        o_cmp = jnp.einsum('bqghn,bngd->bqghd', p_cmp.astype(v_cmp.dtype), v_cmp)
        imp = jnp.einsum('bqgn,nj->bqgj', jnp.sum(p_cmp, axis=3), sel_map)
        tb = (t // SEL_L)[:, None]
        valid = (blk_ids[None, :] <= tb)[None, :, None, :]
        forced = ((blk_ids[None, :] == 0) | (blk_ids[None, :] == tb)
                  | (blk_ids[None, :] == tb - 1))[None, :, None, :]
        score = jnp.where(forced, jnp.inf, jnp.where(valid, imp, -jnp.inf))
        top_val, top_idx = lax.top_k(score, n_sel)
        kg = k_blk[b_ix, g_ix, top_idx]
        vg = v_blk[b_ix, g_ix, top_idx]
        pos = top_idx[..., None] * SEL_L + jnp.arange(SEL_L)
        smask = (top_val > -jnp.inf)[..., None] & (pos <= t[None, :, None, None, None])
        s = jnp.einsum('bqghd,bqgnkd->bqghnk', qb, kg) * scale
        n_tok = n_sel * SEL_L
        p = masked_softmax(s.reshape(B, Q_BLOCK, G, HG, n_tok),
                           smask.reshape(B, Q_BLOCK, G, 1, n_tok))
        o_slc = jnp.einsum('bqghm,bqgmd->bqghd', p.astype(vg.dtype),
                           vg.reshape(B, Q_BLOCK, G, n_tok, HD))
        kw = lax.dynamic_slice_in_dim(k_win, c * Q_BLOCK, Q_BLOCK + WIN, axis=1)
        vw = lax.dynamic_slice_in_dim(v_win, c * Q_BLOCK, Q_BLOCK + WIN, axis=1)
        kp = c * Q_BLOCK - WIN + jnp.arange(Q_BLOCK + WIN)
        wmask = (kp[None, :] <= t[:, None]) & (kp[None, :] > t[:, None] - WIN) & (kp[None, :] >= 0)
        s = jnp.einsum('bqghd,bkgd->bqghk', qb, kw) * scale
        p = masked_softmax(s, wmask[None, :, None, None, :])
        o_win = jnp.einsum('bqghk,bkgd->bqghd', p.astype(vw.dtype), vw)
        o = o_cmp * gb[..., 0:1] + o_slc * gb[..., 1:2] + o_win * gb[..., 2:3]
        return o.reshape(B, Q_BLOCK, NSA_W)

    out = lax.map(block, (jnp.arange(n_qb), qs, gs))
    return out.transpose(1, 0, 2, 3).reshape(B, S, NSA_W)


def setup_inputs(seed: int = 0) -> dict:
    key = jax.random.key(seed)
    ks = iter(jax.random.split(key, 32))

    def w(shape, fan_in):
        return jax.random.normal(next(ks), shape, jnp.float32) * fan_in ** -0.5

    def gain(shape):
        return 1.0 + 0.1 * jax.random.normal(next(ks), shape, jnp.float32)

    def bias(shape):
        return 0.01 * jax.random.normal(next(ks), shape, jnp.float32)

    return {
        'x': jax.random.normal(next(ks), (BATCH, SEQ, D_MODEL), jnp.float32),
        'mem': jax.random.normal(next(ks), (BATCH, MEM_LEN, D_MODEL), jnp.float32),
        'norm_mix_g': gain((DEPTH, D_MODEL)),
        'norm_mlp_g': gain((DEPTH, D_MODEL)),
        'mem_norm_g': gain((D_MODEL,)),
        'w_mem_kv': w((DEPTH, D_MODEL, 2 * MEM_W), D_MODEL),
        'mem_q_norm_g': gain((DEPTH, HEAD_DIM)),
        'mem_k_norm_g': gain((DEPTH, HEAD_DIM)),
        'w_out': w((DEPTH, D_MODEL, D_MODEL), D_MODEL),
        'w_mlp_in': w((DEPTH, D_MODEL, D_FF), D_MODEL),
        'w_mlp_out': w((DEPTH, D_FF, D_MODEL), D_FF),
        'a_w_in': w((N_A_LAYERS, D_MODEL, A_IN_W), D_MODEL),
        'a_b_glu': bias((N_A_LAYERS, 2 * CONV_W)),
        'a_dw': w((N_A_LAYERS, CONV_K, CONV_W), CONV_K),
        'a_dw_b': bias((N_A_LAYERS, CONV_W)),
        'a_ln_g': gain((N_A_LAYERS, CONV_W)),
        'a_ln_b': bias((N_A_LAYERS, CONV_W)),
        'b_w_in': w((N_B_LAYERS, D_MODEL, B_IN_W), D_MODEL),
        'b_gate_b': bias((N_B_LAYERS, 3 * NSA_HEADS)),
        'b_q_norm_g': gain((N_B_LAYERS, HEAD_DIM)),
        'kv_norm_g': gain((D_MODEL,)),
        'w_kv': w((D_MODEL, KV_W), D_MODEL),
        'k_norm_g': gain((3, HEAD_DIM)),
        'cmp_pe_k': 0.1 * jax.random.normal(next(ks), (CMP_L, HEAD_DIM), jnp.float32),
        'cmp_pe_v': 0.1 * jax.random.normal(next(ks), (CMP_L, HEAD_DIM), jnp.float32),
        'cmp_w1_k': w((CMP_L * HEAD_DIM, CMP_HID), CMP_L * HEAD_DIM),
        'cmp_w2_k': w((CMP_HID, HEAD_DIM), CMP_HID),
        'cmp_w1_v': w((CMP_L * HEAD_DIM, CMP_HID), CMP_L * HEAD_DIM),
        'cmp_w2_v': w((CMP_HID, HEAD_DIM), CMP_HID),
    }


def reference(x, mem, norm_mix_g, norm_mlp_g, mem_norm_g, w_mem_kv, mem_q_norm_g, mem_k_norm_g,
              w_out, w_mlp_in, w_mlp_out, a_w_in, a_b_glu, a_dw, a_dw_b, a_ln_g, a_ln_b,
              b_w_in, b_gate_b, b_q_norm_g, kv_norm_g, w_kv, k_norm_g, cmp_pe_k, cmp_pe_v,
              cmp_w1_k, cmp_w2_k, cmp_w1_v, cmp_w2_v):
    B, S = x.shape[:2]
    mem_n = rms_norm(mem, mem_norm_g)
    shared = None
    for l in range(DEPTH):
        mkv = (mem_n @ w_mem_kv[l]).reshape(B, MEM_LEN, 2, MEM_HEADS, HEAD_DIM)
        mk = rms_norm(mkv[:, :, 0], mem_k_norm_g[l])
        mv = mkv[:, :, 1]
        h = rms_norm(x, norm_mix_g[l])
        if l == N_A_LAYERS:
            shared = shared_nsa_kv(x, kv_norm_g, w_kv, k_norm_g, cmp_pe_k, cmp_pe_v,
                                   cmp_w1_k, cmp_w2_k, cmp_w1_v, cmp_w2_v)
        if l < N_A_LAYERS:
            i = l
            u = h @ a_w_in[i]
            conv_out = conformer_conv(u[..., :2 * CONV_W] + a_b_glu[i], a_dw[i], a_dw_b[i],
                                      a_ln_g[i], a_ln_b[i])
            qm = rms_norm(u[..., 2 * CONV_W:].reshape(B, S, MEM_HEADS, HEAD_DIM), mem_q_norm_g[l])
            mix = jnp.concatenate([conv_out, memory_attention(qm, mk, mv)], axis=-1)
        else:
            i = l - N_A_LAYERS
            u = h @ b_w_in[i]
            q = rms_norm(u[..., :NSA_W].reshape(B, S, NSA_HEADS, HEAD_DIM), b_q_norm_g[i])
            qm = rms_norm(u[..., NSA_W:NSA_W + MEM_W].reshape(B, S, MEM_HEADS, HEAD_DIM),
                          mem_q_norm_g[l])
            gates = jax.nn.sigmoid(u[..., NSA_W + MEM_W:] + b_gate_b[i]).reshape(B, S, NSA_HEADS, 3)
            nsa_out = nsa_attention(q, gates, *shared)
            mix = jnp.concatenate([nsa_out, memory_attention(qm, mk, mv)], axis=-1)
        x = x + mix @ w_out[l]
        x = x + squared_relu_mlp(rms_norm(x, norm_mlp_g[l]), w_mlp_in[l], w_mlp_out[l])
    return x
```

```python
import contextlib
import numpy as np
import ml_dtypes
import concourse.bass as bass
import concourse.mybir as mybir
from concourse.bass_utils import run_bass_kernel_spmd

F32 = mybir.dt.float32
BF16 = mybir.dt.bfloat16
AF = mybir.ActivationFunctionType
ALU = mybir.AluOpType
AX = mybir.AxisListType

NCORES = 8
S = 16384
D = 1024
TPC = S // NCORES
HALO = 32
NEG = -32768.0
TEST_STAGE = 99


class Buf:
    __slots__ = ("t", "w", "r", "name")

    def __init__(self, t, name):
        self.t = t
        self.w = None
        self.r = []
        self.name = name

    def __getitem__(self, idx):
        return self.t[idx]


class Sched:
    def __init__(self, nc, es):
        self.nc = nc
        self.es = es
        self.eng = {"pe": nc.tensor, "act": nc.scalar, "dve": nc.vector, "pool": nc.gpsimd, "sp": nc.sync}
        self.sem = {}
        self.cnt = {}
        for k in ("pe", "act", "dve", "pool"):
            self.sem[k] = es.enter_context(nc.semaphore("s_" + k))
            self.cnt[k] = 0
        self.dsem = {}
        self.waited = {}
        self.nbuf = 0
        self.psum_rr = 0

    def sb(self, shape, dt, name=None):
        self.nbuf += 1
        name = f"{name or 'b'}_{self.nbuf}"
        t = self.es.enter_context(self.nc.sbuf_tensor(name, list(shape), dt))
        return Buf(t, name)

    def psum(self, shape, dt, name=None):
        self.nbuf += 1
        name = f"{name or 'p'}_{self.nbuf}"
        t = self.es.enter_context(self.nc.psum_tensor(name, list(shape), dt))
        return Buf(t, name)

    def dram(self, name, shape, dt, kind):
        t = self.nc.dram_tensor(name, list(shape), dt, kind=kind)
        return Buf(t.ap(), name)

    def dma_stream(self, name):
        if name not in self.dsem:
            self.dsem[name] = self.es.enter_context(self.nc.semaphore("d_" + name))
            self.cnt["d_" + name] = 0
        return name

    def _tok_sem(self, tok):
        k = tok[0]
        return self.sem[k] if k in self.sem else self.dsem[k[2:]]

    def _wait(self, eng, toks):
        need = {}
        for tok in toks:
            if tok is None:
                continue
            k, v = tok
            if k == "pe" and eng == "pe":
                continue
            if k.startswith("d_"):
                v = self.cnt[k]
            if need.get(k, 0) < v:
                need[k] = v
        for k, v in need.items():
            if self.waited.get((eng, k), 0) >= v:
                continue
            self.waited[(eng, k)] = v
            self.eng[eng].wait_ge(self._tok_sem((k, v)), v)

    def _deps(self, reads, writes):
        toks = []
        for b in reads:
            toks.append(b.w)
        for b in writes:
            toks.append(b.w)
            toks.extend(b.r)
        return toks

    def _commit(self, tok, reads, writes):
        for b in reads:
            b.r.append(tok)
        for b in writes:
            b.w = tok
            b.r = []

    def op(self, eng, fn, reads=(), writes=()):
        self._wait(eng, self._deps(reads, writes))
        inst = fn(self.eng[eng])
        inst.then_inc(self.sem[eng], 1)
        self.cnt[eng] += 1
        tok = (eng, self.cnt[eng])
        self._commit(tok, reads, writes)
        return tok

    def dma(self, q, stream, out, in_, reads=(), writes=()):
        self.dma_stream(stream)
        self._wait(q, self._deps(reads, writes))
        inst = self.eng[q].dma_start(out=out, in_=in_)
        inst.then_inc(self.dsem[stream], 16)
        k = "d_" + stream
        self.cnt[k] += 16
        tok = (k, self.cnt[k])
        self._commit(tok, reads, writes)
        return tok

    def barrier(self):
        toks = [(k, v) for k, v in self.cnt.items() if v > 0]
        for e in ("pe", "act", "dve", "pool", "sp"):
            need = [t for t in toks if not (t[0] == e)]
            self._wait_raw(e, need)

    def _wait_raw(self, eng, toks):
        for k, v in toks:
            if self.waited.get((eng, k), 0) >= v:
                continue
            self.waited[(eng, k)] = v
            self.eng[eng].wait_ge(self._tok_sem((k, v)), v)

    def finish(self, bufs):
        toks = []
        for b in bufs:
            toks.append(b.w)
            toks.extend(b.r)
        self._wait("sp", toks)


class Ctx:
    pass


def mm(sc, out_b, out_ap, lhsT_b, lhsT_ap, rhs_b, rhs_ap, start, stop, skip=False):
    sc.op("pe", lambda e: e.matmul(out_ap, lhsT=lhsT_ap, rhs=rhs_ap, start=start, stop=stop,
                                   skip_group_check=skip),
          reads=[lhsT_b, rhs_b], writes=[out_b])


def rstd_from_ms(sc, ms_b, ms_ap, out_b, out_ap, tmp_b, tmp_ap, inv_n, eps, square=False):
    if square:
        sc.op("act", lambda e: e.activation(out=tmp_ap, in_=ms_ap, func=AF.Identity, scale=inv_n, bias=eps),
              reads=[ms_b], writes=[tmp_b])
    else:
        sc.op("act", lambda e: e.activation(out=tmp_ap, in_=ms_ap, func=AF.Sqrt, scale=inv_n, bias=eps),
              reads=[ms_b], writes=[tmp_b])
    sc.op("dve", lambda e: e.reciprocal(out=out_ap, in_=tmp_ap), reads=[tmp_b], writes=[out_b])


class PsumPool:
    def __init__(self, sc, n, shape=(128, 512), dt=F32, name="ps"):
        self.bufs = [sc.psum(shape, dt, f"{name}{i}") for i in range(n)]
        self.i = 0

    def get(self):
        b = self.bufs[self.i % len(self.bufs)]
        self.i += 1
        return b


class SbPool:
    def __init__(self, sc, n, shape, dt, name):
        self.bufs = [sc.sb(shape, dt, f"{name}{i}") for i in range(n)]
        self.i = 0

    def get(self):
        b = self.bufs[self.i % len(self.bufs)]
        self.i += 1
        return b


def load_weight_bf16(sc, cx, w_dram, rows, col0, ncols, dst_b, dst_ap_fn, gcol_fn=None, row0=0):
    nk = rows // 128
    for kc in range(nk):
        st = cx.wstage.get()
        src = w_dram.t[row0 + kc * 128: row0 + (kc + 1) * 128, col0:col0 + ncols]
        sc.dma("sp", "w" + st.name, st[:, 0:ncols], src, reads=[w_dram], writes=[st])
        dst = dst_ap_fn(kc)
        eng = cx.cast_engs[cx.cast_i % len(cx.cast_engs)]
        cx.cast_i += 1
        if gcol_fn is not None:
            g = gcol_fn(kc)
            if eng == "act":
                sc.op("act", lambda e: e.activation(out=dst, in_=st[:, 0:ncols], func=AF.Copy, scale=g),
                      reads=[st, cx.vecs], writes=[dst_b])
            else:
                sc.op(eng, lambda e: e.tensor_scalar(out=dst, in0=st[:, 0:ncols], scalar1=g, scalar2=1.0,
                                                     op0=ALU.mult, op1=ALU.mult),
                      reads=[st, cx.vecs], writes=[dst_b])
        else:
            if eng == "act":
                sc.op("act", lambda e: e.activation(out=dst, in_=st[:, 0:ncols], func=AF.Copy),
                      reads=[st], writes=[dst_b])
            else:
                sc.op(eng, lambda e: e.tensor_copy(out=dst, in_=st[:, 0:ncols]), reads=[st], writes=[dst_b])


def sumsq_bc(sc, cx, src_b, src_ap_fn, nchunks, n, lhsT_ap, ps_b, ps_ap):
    for c in range(nchunks):
        sq = cx.sqpool.get()
        src = src_ap_fn(c)
        sc.op("act", lambda e: e.activation(out=sq[:, 0:n], in_=src, func=AF.Square), reads=[src_b], writes=[sq])
        mm(sc, ps_b, ps_ap, cx.consts, lhsT_ap, sq, sq[:, 0:n], start=(c == 0), stop=(c == nchunks - 1))


def vec_layout():
    off = {}
    o = 0

    def add(name, n):
        nonlocal o
        off[name] = o
        o += n
    for l in range(2):
        add(f"gmix{l}", 8)
        add(f"gmlp{l}", 8)
        add(f"gq_mem{l}", 1)
        add(f"gk_mem{l}", 1)
    add("gkv", 8)
    add("gmem", 8)
    add("b_glu", 12)
    add("dw", 6 * 31)
    add("dw_b", 6)
    add("ln_g", 6)
    add("ln_b", 6)
    add("gk_s", 1)
    add("gk_w", 1)
    add("gk_c", 1)
    add("gq_nsa", 1)
    add("halo", 1)
    add("eps6", 1)
    add("eps5", 1)
    off["_n"] = o
    return off


VOFF = vec_layout()


def pack_vecs(inp, core):
    v = np.zeros((128, VOFF["_n"]), np.float32)

    def chunks(name, arr):
        a = np.asarray(arr, np.float32)
        n = a.shape[0] // 128
        v[:, VOFF[name]:VOFF[name] + n] = a.reshape(n, 128).T

    def tile64(name, arr):
        a = np.asarray(arr, np.float32)
        v[:, VOFF[name]] = np.concatenate([a, a])
    for l in range(2):
        chunks(f"gmix{l}", inp["norm_mix_g"][l])
        chunks(f"gmlp{l}", inp["norm_mlp_g"][l])
        tile64(f"gq_mem{l}", inp["mem_q_norm_g"][l])
        tile64(f"gk_mem{l}", inp["mem_k_norm_g"][l])
    chunks("gkv", inp["kv_norm_g"])
    chunks("gmem", inp["mem_norm_g"])
    chunks("b_glu", inp["a_b_glu"][0])
    dw = np.asarray(inp["a_dw"][0], np.float32)
    for c in range(6):
        v[:, VOFF["dw"] + c * 31: VOFF["dw"] + (c + 1) * 31] = dw[:, c * 128:(c + 1) * 128].T
    chunks("dw_b", inp["a_dw_b"][0])
    chunks("ln_g", inp["a_ln_g"][0])
    chunks("ln_b", inp["a_ln_b"][0])
    tile64("gk_c", inp["k_norm_g"][0])
    tile64("gk_s", inp["k_norm_g"][1])
    tile64("gk_w", inp["k_norm_g"][2])
    tile64("gq_nsa", inp["b_q_norm_g"][0])
    v[:, VOFF["halo"]] = 0.0 if core == 0 else 1.0
    v[:, VOFF["eps6"]] = 1e-6
    v[:, VOFF["eps5"]] = 1e-5
    return v


def make_consts():
    c = np.zeros((128, 384), np.float32)
    c[:, 0:128] = np.eye(128)
    c[:, 128:256] = 1.0
    p = np.arange(128)
    c[:, 256:384] = (p[:, None] // 64 == p[None, :] // 64)
    return c.astype(ml_dtypes.bfloat16)


def emit_mem_kv(sc, cx, memT_d, wmem_d, layer, mkT, mvp):
    V = cx.vecs
    vo = VOFF
    ident, ones, bones = cx.ident, cx.ones, cx.bones
    memx = sc.sb((128, 8, 256), F32, "memx")
    for c in range(8):
        sc.dma("sp", "memx", memx[:, c, :], memT_d.t[c * 128:(c + 1) * 128, :], reads=[memT_d], writes=[memx])
    ps = cx.pp.get()
    sumsq_bc(sc, cx, memx, lambda c: memx[:, c, :], 8, 256, ones, ps, ps[:, 0:256])
    rs = sc.sb((128, 256), F32, "mem_rs")
    tmp = sc.sb((128, 256), F32, "mem_tmp")
    rstd_from_ms(sc, ps, ps[:, 0:256], rs, rs[:, :], tmp, tmp[:, :], 1.0 / D, 1e-6)
    memn = sc.sb((128, 8, 256), BF16, "memn")
    for c in range(8):
        g = V[:, vo["gmem"] + c: vo["gmem"] + c + 1]
        sc.op("dve", lambda e: e.scalar_tensor_tensor(out=memn[:, c, :], in0=memx[:, c, :], scalar=g, in1=rs[:, :],
                                                      op0=ALU.mult, op1=ALU.mult),
              reads=[memx, V, rs], writes=[memn])
    wm = sc.sb((128, 8, 512), BF16, "wmem")
    load_weight_bf16(sc, cx, wmem_d, 1024, 0, 512, wm, lambda kc: wm[:, kc, :])
    for fc in range(2):
        ps = cx.pp.get()
        for kc in range(8):
            mm(sc, ps, ps[:, 0:256], wm, wm[:, kc, fc * 128:(fc + 1) * 128], memn, memn[:, kc, :], kc == 0, kc == 7)
        kf = sc.sb((128, 256), F32, f"mk_f{fc}")
        sc.op("act", lambda e: e.activation(out=kf[:, :], in_=ps[:, 0:256], func=AF.Copy), reads=[ps], writes=[kf])
        ps2 = cx.pp.get()
        sumsq_bc(sc, cx, kf, lambda c: kf[:, :], 1, 256, bones, ps2, ps2[:, 0:256])
        rh = sc.sb((128, 256), F32, f"mk_rh{fc}")
        rstd_from_ms(sc, ps2, ps2[:, 0:256], rh, rh[:, :], tmp, tmp[:, :], 1.0 / 64, 1e-6)
        g = V[:, vo[f"gk_mem{layer}"]: vo[f"gk_mem{layer}"] + 1]
        sc.op("dve", lambda e: e.scalar_tensor_tensor(out=mkT[:, fc, :], in0=kf[:, :], scalar=g, in1=rh[:, :],
                                                      op0=ALU.mult, op1=ALU.mult),
              reads=[kf, V, rh], writes=[mkT])
    sc.op("pool", lambda e: e.memset(mvp[:, :, :, 64:65], 1.0), writes=[mvp])
    for mt in range(2):
        ps = cx.pp.get()
        for kc in range(8):
            mm(sc, ps, ps[:, 0:256], memn, memn[:, kc, mt * 128:(mt + 1) * 128], wm, wm[:, kc, 256:512], kc == 0, kc == 7)
        sc.op("act", lambda e: e.activation(out=mvp[:, mt, :, 0:64],
                                            in_=ps[:, 0:256].rearrange("p (h d) -> p h d", h=4), func=AF.Copy),
              reads=[ps], writes=[mvp])


def emit_mem_attn(sc, cx, qmT, qcol0, mkT, mvp, mixtok, mixcol0):
    pTs = []
    for mt in range(2):
        pT = cx.ptpool.get()
        for par in range(2):
            ps = cx.pp.get()
            b = par * 64
            for ch in range(2):
                mm(sc, ps, ps[:, ch * 128:(ch + 1) * 128], mkT, mkT[b:b + 64, ch, mt * 128:(mt + 1) * 128],
                   qmT, qmT[b:b + 64, ch, qcol0:qcol0 + 128], True, True)
            sc.op("act", lambda e: e.activation(out=pT[:, par * 256:(par + 1) * 256], in_=ps[:, 0:256], func=AF.Exp, scale=0.125),
                  reads=[ps], writes=[pT])
        pTs.append(pT)
    if TEST_STAGE < 2:
        return
    acc = cx.pp.get()
    for h in range(4):
        for mt in range(2):
            pc = (h % 2) * 256 + (h // 2) * 128
            mm(sc, acc, acc[:, h * 65:(h + 1) * 65], pTs[mt], pTs[mt][:, pc:pc + 128],
               mvp, mvp[:, mt, h, :], mt == 0, mt == 1)
    if TEST_STAGE < 3:
        return
    accv = acc[:, 0:260].rearrange("p (h e) -> p h e", h=4)
    r = cx.small.get()
    sc.op("dve", lambda e: e.tensor_scalar(out=r[:, 0:4], in0=accv[:, :, 64], scalar1=1e-30, scalar2=None, op0=ALU.max),
          reads=[acc], writes=[r])
    sc.op("dve", lambda e: e.reciprocal(out=r[:, 4:8], in_=r[:, 0:4]), reads=[r], writes=[r])
    if TEST_STAGE < 4:
        return
    for h in range(4):
        sc.op("dve", lambda e: e.tensor_scalar(out=mixtok[:, mixcol0 + h * 64: mixcol0 + (h + 1) * 64],
                                               in0=acc[:, h * 65: h * 65 + 64], scalar1=r[:, 4 + h: 5 + h], scalar2=None,
                                               op0=ALU.mult),
              reads=[acc, r], writes=[mixtok])


def emit_transpose_to(sc, cx, src_b, src_ap, dst_b, dst_ap):
    pt = cx.ptr.get()
    sc.op("pe", lambda e: e.transpose(pt[:, 0:128], src_ap, cx.ident), reads=[src_b, cx.consts], writes=[pt])
    sc.op("act", lambda e: e.activation(out=dst_ap, in_=pt[:, 0:128], func=AF.Copy), reads=[pt], writes=[dst_b])


def emit_mlp(sc, cx, xts, win_d, wout_d, gname):
    V = cx.vecs
    ones = cx.ones
    NT = len(xts)
    rstd2 = [sc.sb((128, 512), F32, f"mlp_rstd2_{i}") for i in range(NT)]
    xbs = [sc.sb((128, 8, 512), BF16, f"mlp_xb{i}") for i in range(NT)]
    tmp = sc.sb((128, 512), F32, "mlp_tmp")
    for tt in range(NT):
        xT = xts[tt]
        ps = cx.pp.get()
        sumsq_bc(sc, cx, xT, lambda c: xT[:, c, :], 8, 512, ones, ps, ps[:, :])
        rstd_from_ms(sc, ps, ps[:, :], rstd2[tt], rstd2[tt][:, :], tmp, tmp[:, :], 1.0 / D, 1e-6, square=True)
        for c in range(8):
            sc.op("pool", lambda e: e.tensor_copy(out=xbs[tt][:, c, :], in_=xT[:, c, :]), reads=[xT], writes=[xbs[tt]])
    win_p = SbPool(sc, 2, (128, 8, 512), BF16, "mlp_win")
    wout_p = SbPool(sc, 2, (128, 4, 1024), BF16, "mlp_wout")
    a_p = SbPool(sc, 2, (128, 4, 512), BF16, "mlp_a")
    r_p = SbPool(sc, 2, (128, 512), BF16, "mlp_r")
    t_p = SbPool(sc, 2, (128, 512), F32, "mlp_t")
    for hg in range(8):
        win = win_p.get()
        wout = wout_p.get()
        load_weight_bf16(sc, cx, win_d, 1024, hg * 512, 512, win, lambda kc: win[:, kc, :],
                         gcol_fn=lambda kc: V[:, VOFF[gname] + kc: VOFF[gname] + kc + 1])
        load_weight_bf16(sc, cx, wout_d, 512, 0, 1024, wout, lambda kc: wout[:, kc, :], row0=hg * 512)
        for tt in range(NT):
            xT = xts[tt]
            xb = xbs[tt]
            a = a_p.get()
            for fc in range(4):
                ps = cx.pp.get()
                for kc in range(8):
                    mm(sc, ps, ps[:, :], win, win[:, kc, fc * 128:(fc + 1) * 128], xb, xb[:, kc, :], kc == 0, kc == 7)
                r = r_p.get()
                sc.op("act", lambda e: e.activation(out=r[:, :], in_=ps[:, :], func=AF.Relu), reads=[ps], writes=[r])
                sc.op("pool", lambda e: e.tensor_tensor(out=a[:, fc, :], in0=r[:, :], in1=r[:, :], op=ALU.mult),
                      reads=[r], writes=[a])
            for oc in range(8):
                ps = cx.pp.get()
                for fc in range(4):
                    mm(sc, ps, ps[:, :], wout, wout[:, fc, oc * 128:(oc + 1) * 128], a, a[:, fc, :], fc == 0, fc == 3)
                t = t_p.get()
                sc.op("dve", lambda e: e.tensor_tensor(out=t[:, :], in0=ps[:, :], in1=rstd2[tt][:, :], op=ALU.mult),
                      reads=[ps, rstd2[tt]], writes=[t])
                sc.op("pool", lambda e: e.tensor_tensor(out=xT[:, oc, :], in0=xT[:, oc, :], in1=t[:, :], op=ALU.add),
                      reads=[t, xT], writes=[xT])


def common_setup(sc, nc, cx, npp=6):
    cx.consts_d = sc.dram("consts", (128, 384), BF16, "ExternalInput")
    cx.vecs_d = sc.dram("vecs", (128, VOFF["_n"]), F32, "ExternalInput")
    cx.consts = sc.sb((128, 384), BF16, "consts_sb")
    cx.vecs = sc.sb((128, VOFF["_n"]), F32, "vecs_sb")
    sc.dma("sp", "c0", cx.consts[:, :], cx.consts_d.t[:, :], reads=[cx.consts_d], writes=[cx.consts])
    sc.dma("sp", "c1", cx.vecs[:, :], cx.vecs_d.t[:, :], reads=[cx.vecs_d], writes=[cx.vecs])
    cx.ident = cx.consts[:, 0:128]
    cx.ones = cx.consts[:, 128:256]
    cx.bones = cx.consts[:, 256:384]
    cx.pp = PsumPool(sc, npp)
    cx.ptr = PsumPool(sc, 2, (128, 512), BF16, "ptr")
    cx.wstage = SbPool(sc, 2, (128, 1024), F32, "wst")
    cx.sqpool = SbPool(sc, 3, (128, 512), BF16, "sq")
    cx.ptpool = SbPool(sc, 3, (128, 512), BF16, "pT")
    cx.small = SbPool(sc, 4, (128, 16), F32, "small")
    cx.cast_engs = ["act", "pool"]
    cx.cast_i = 0


class Scope:
    def __init__(self, sc):
        self.sc = sc

    def __enter__(self):
        self.saved = self.sc.es
        self.es = contextlib.ExitStack()
        self.es.__enter__()
        self.sc.es = self.es
        return self

    def __exit__(self, *a):
        self.sc.barrier()
        self.sc.es = self.saved
        return self.es.__exit__(*a)


TM = 256


def build_l0(stage=99):
    nc = bass.Bass("TRN2", target_bir_lowering=False)
    es = contextlib.ExitStack()
    with es:
        sc = Sched(nc, es)
        cx = Ctx()
        xT_d = sc.dram("xT", (D, HALO + TPC), F32, "ExternalInput")
        memT_d = sc.dram("memT", (D, 256), F32, "ExternalInput")
        awin_d = sc.dram("a_w_in", (D, 1792), F32, "ExternalInput")
        wout_d = sc.dram("w_out", (D, D), F32, "ExternalInput")
        wmi_d = sc.dram("w_mlp_in", (D, 4096), F32, "ExternalInput")
        wmo_d = sc.dram("w_mlp_out", (4096, D), F32, "ExternalInput")
        wmem_d = sc.dram("w_mem_kv", (D, 512), F32, "ExternalInput")
        wkv_d = sc.dram("w_kv", (D, 768), F32, "ExternalInput")
        x1T_d = sc.dram("x1T", (D, TPC), F32, "ExternalOutput")
        kT_d = sc.dram("kT", (4, 128, TPC), BF16, "ExternalOutput")
        vtok_d = sc.dram("vtok", (2, TPC, 128), BF16, "ExternalOutput")
        common_setup(sc, nc, cx)
        V = cx.vecs
        vo = VOFF
        ones, bones = cx.ones, cx.bones

        xts = [sc.sb((128, 8, 512), F32, f"xt{i}") for i in range(TPC // 512)]
        xh = sc.sb((128, 8, HALO), F32, "xh")
        for c in range(8):
            sc.dma("sp", "xh", xh[:, c, :], xT_d.t[c * 128:(c + 1) * 128, 0:HALO], reads=[xT_d], writes=[xh])
        for i, xt in enumerate(xts):
            for c in range(8):
                sc.dma("sp", f"x{i}", xt[:, c, :], xT_d.t[c * 128:(c + 1) * 128, HALO + i * 512: HALO + (i + 1) * 512],
                       reads=[xT_d], writes=[xt])

        with Scope(sc):
            mkT = sc.sb((128, 2, 256), BF16, "mkT")
            mvp = sc.sb((128, 2, 4, 65), BF16, "mvp")
            with Scope(sc):
                if stage >= 1:
                    emit_mem_kv(sc, cx, memT_d, wmem_d, 0, mkT, mvp)
            wa = sc.sb((128, 8, 1792), BF16, "wa")
            for p in range(2):
                load_weight_bf16(sc, cx, awin_d, 1024, p * 896, 896, wa, lambda kc: wa[:, kc, p * 896:(p + 1) * 896],
                                 gcol_fn=lambda kc: V[:, vo["gmix0"] + kc: vo["gmix0"] + kc + 1])
            wo = sc.sb((128, 8, 1024), BF16, "wo")
            load_weight_bf16(sc, cx, wout_d, 1024, 0, 1024, wo, lambda kc: wo[:, kc, :])

            vh = sc.sb((128, 6, HALO + TM), F32, "vh")
            vcarry = sc.sb((128, 6, HALO), F32, "vcarry")
            xb = sc.sb((128, 8, TM), BF16, "xb")
            rstd = sc.sb((128, TM), F32, "rstd")
            tmp = sc.sb((128, TM), F32, "tmp")
            ta_p = SbPool(sc, 2, (128, TM), F32, "ta")
            tg_p = SbPool(sc, 2, (128, TM), F32, "tg")
            cvs = [sc.sb((128, TM), F32, f"cv{c}") for c in range(6)]
            cvb_p = SbPool(sc, 2, (128, TM), BF16, "cvb")
            mu = sc.sb((128, TM), F32, "mu")
            musq = sc.sb((128, TM), F32, "musq")
            var = sc.sb((128, TM), F32, "var")
            rl = sc.sb((128, TM), F32, "rl")
            y_p = SbPool(sc, 2, (128, TM), F32, "y")
            mixT_p = SbPool(sc, 2, (128, 8, TM), BF16, "mixT")
            qf = sc.sb((128, 2, TM), F32, "qf")
            qmT = sc.sb((128, 2, TM), BF16, "qmT")
            rh = sc.sb((128, TM), F32, "rh")
            mixtok_p = SbPool(sc, 2, (128, 256), BF16, "mixtok")

            def glu_block(xsrc, xcs, n, vcol0, halo):
                ps = cx.pp.get()
                sumsq_bc(sc, cx, xsrc, lambda c: xsrc[:, c, xcs], 8, n, ones, ps, ps[:, 0:n])
                rstd_from_ms(sc, ps, ps[:, 0:n], rstd, rstd[:, 0:n], tmp, tmp[:, 0:n], 1.0 / D, 1e-6)
                for c in range(8):
                    sc.op("pool", lambda e: e.tensor_copy(out=xb[:, c, 0:n], in_=xsrc[:, c, xcs]), reads=[xsrc], writes=[xb])
                for fc in range(6):
                    psa = cx.pp.get()
                    psg = cx.pp.get()
                    for kc in range(8):
                        mm(sc, psa, psa[:, 0:n], wa, wa[:, kc, fc * 128:(fc + 1) * 128], xb, xb[:, kc, 0:n], kc == 0, kc == 7)
                    for kc in range(8):
                        mm(sc, psg, psg[:, 0:n], wa, wa[:, kc, (fc + 6) * 128:(fc + 7) * 128], xb, xb[:, kc, 0:n], kc == 0, kc == 7)
                    ta = ta_p.get()
                    tg = tg_p.get()
                    sc.op("dve", lambda e: e.tensor_tensor(out=ta[:, 0:n], in0=psa[:, 0:n], in1=rstd[:, 0:n], op=ALU.mult),
                          reads=[psa, rstd], writes=[ta])
                    sc.op("dve", lambda e: e.tensor_tensor(out=tg[:, 0:n], in0=psg[:, 0:n], in1=rstd[:, 0:n], op=ALU.mult),
                          reads=[psg, rstd], writes=[tg])
                    bg = V[:, vo["b_glu"] + 6 + fc: vo["b_glu"] + 7 + fc]
                    ba = V[:, vo["b_glu"] + fc: vo["b_glu"] + fc + 1]
                    sc.op("act", lambda e: e.activation(out=tg[:, 0:n], in_=tg[:, 0:n], func=AF.Sigmoid, bias=bg),
                          reads=[tg, V], writes=[tg])
                    sc.op("dve", lambda e: e.scalar_tensor_tensor(out=vh[:, fc, vcol0:vcol0 + n], in0=ta[:, 0:n], scalar=ba,
                                                                  in1=tg[:, 0:n], op0=ALU.add, op1=ALU.mult),
                          reads=[ta, V, tg], writes=[vh])
                    if halo:
                        hs = V[:, vo["halo"]: vo["halo"] + 1]
                        sc.op("dve", lambda e: e.tensor_scalar(out=vh[:, fc, vcol0:vcol0 + n], in0=vh[:, fc, vcol0:vcol0 + n],
                                                               scalar1=hs, scalar2=None, op0=ALU.mult),
                              reads=[vh, V], writes=[vh])

            for tt in range((TPC // TM) if stage >= 3 else (1 if stage >= 2 else 0)):
                xt = xts[(tt * TM) // 512]
                cs = slice((tt * TM) % 512, (tt * TM) % 512 + TM)
                if tt == 0:
                    glu_block(xh, slice(0, HALO), HALO, 0, True)
                else:
                    for c in range(6):
                        sc.op("pool", lambda e: e.tensor_copy(out=vh[:, c, 0:HALO], in_=vcarry[:, c, :]),
                              reads=[vcarry], writes=[vh])
                glu_block(xt, cs, TM, HALO, False)
                for c in range(6):
                    sc.op("pool", lambda e: e.tensor_copy(out=vcarry[:, c, :], in_=vh[:, c, TM:TM + HALO]),
                          reads=[vh], writes=[vcarry])
                if stage < 2.1:
                    continue
                for j in range(2):
                    ps = cx.pp.get()
                    for kc in range(8):
                        mm(sc, ps, ps[:, 0:TM], wa, wa[:, kc, (12 + j) * 128:(13 + j) * 128], xb, xb[:, kc, :], kc == 0, kc == 7)
                    sc.op("dve", lambda e: e.tensor_tensor(out=qf[:, j, :], in0=ps[:, 0:TM], in1=rstd[:, :], op=ALU.mult),
                          reads=[ps, rstd], writes=[qf])
                    ps2 = cx.pp.get()
                    sumsq_bc(sc, cx, qf, lambda c: qf[:, j, :], 1, TM, bones, ps2, ps2[:, 0:TM])
                    rstd_from_ms(sc, ps2, ps2[:, 0:TM], rh, rh[:, :], tmp, tmp[:, :], 1.0 / 64, 1e-6)
                    g = V[:, vo["gq_mem0"]: vo["gq_mem0"] + 1]
                    sc.op("dve", lambda e: e.scalar_tensor_tensor(out=qmT[:, j, :], in0=qf[:, j, :], scalar=g, in1=rh[:, :],
                                                                  op0=ALU.mult, op1=ALU.mult),
                          reads=[qf, V, rh], writes=[qmT])
                if stage < 2.2:
                    continue
                for c in range(6):
                    dwc = vo["dw"] + c * 31
                    sc.op("dve", lambda e: e.tensor_scalar(out=cvs[c][:, :], in0=vh[:, c, 2:2 + TM], scalar1=V[:, dwc:dwc + 1],
                                                           scalar2=V[:, vo["dw_b"] + c: vo["dw_b"] + c + 1],
                                                           op0=ALU.mult, op1=ALU.add),
                          reads=[vh, V], writes=[cvs[c]])
                for k in range(1, 31):
                    for c in range(6):
                        dwc = vo["dw"] + c * 31
                        sc.op("dve", lambda e: e.scalar_tensor_tensor(out=cvs[c][:, :], in0=vh[:, c, 2 + k:2 + k + TM],
                                                                      scalar=V[:, dwc + k:dwc + k + 1], in1=cvs[c][:, :],
                                                                      op0=ALU.mult, op1=ALU.add),
                              reads=[vh, V, cvs[c]], writes=[cvs[c]])
                if stage < 2.3:
                    continue
                ps1 = cx.pp.get()
                ps2 = cx.pp.get()
                for c in range(6):
                    cvb = cvb_p.get()
                    sc.op("act", lambda e: e.activation(out=cvb[:, :], in_=cvs[c][:, :], func=AF.Copy), reads=[cvs[c]], writes=[cvb])
                    mm(sc, ps1, ps1[:, 0:TM], cx.consts, ones, cvb, cvb[:, :], c == 0, c == 5)
                    sq = cx.sqpool.get()
                    sc.op("act", lambda e: e.activation(out=sq[:, 0:TM], in_=cvs[c][:, :], func=AF.Square), reads=[cvs[c]], writes=[sq])
                    mm(sc, ps2, ps2[:, 0:TM], cx.consts, ones, sq, sq[:, 0:TM], c == 0, c == 5)
                sc.op("act", lambda e: e.activation(out=mu[:, :], in_=ps1[:, 0:TM], func=AF.Copy, scale=1.0 / 768),
                      reads=[ps1], writes=[mu])
                sc.op("dve", lambda e: e.tensor_tensor(out=musq[:, :], in0=mu[:, :], in1=mu[:, :], op=ALU.mult),
                      reads=[mu], writes=[musq])
                sc.op("dve", lambda e: e.scalar_tensor_tensor(out=var[:, :], in0=ps2[:, 0:TM], scalar=1.0 / 768, in1=musq[:, :],
                                                              op0=ALU.mult, op1=ALU.subtract),
                      reads=[ps2, musq], writes=[var])
                rstd_from_ms(sc, var, var[:, :], rl, rl[:, :], tmp, tmp[:, :], 1.0, 1e-5)
                mixT = mixT_p.get()
                for c in range(6):
                    y = y_p.get()
                    sc.op("dve", lambda e: e.tensor_tensor(out=y[:, :], in0=cvs[c][:, :], in1=mu[:, :], op=ALU.subtract),
                          reads=[cvs[c], mu], writes=[y])
                    sc.op("dve", lambda e: e.tensor_tensor(out=y[:, :], in0=y[:, :], in1=rl[:, :], op=ALU.mult),
                          reads=[y, rl], writes=[y])
                    sc.op("act", lambda e: e.activation(out=mixT[:, c, :], in_=y[:, :], func=AF.Silu,
                                                        scale=V[:, vo["ln_g"] + c: vo["ln_g"] + c + 1],
                                                        bias=V[:, vo["ln_b"] + c: vo["ln_b"] + c + 1]),
                          reads=[y, V], writes=[mixT])
                if stage < 2.4:
                    continue
                for sb_ in range(TM // 128):
                    mixtok = mixtok_p.get()
                    emit_mem_attn(sc, cx, qmT, sb_ * 128, mkT, mvp, mixtok, 0)
                    for j in range(2):
                        emit_transpose_to(sc, cx, mixtok, mixtok[:, j * 128:(j + 1) * 128], mixT,
                                          mixT[:, 6 + j, sb_ * 128:(sb_ + 1) * 128])
                if stage < 2.5:
                    continue
                for oc in range(8):
                    ps = cx.pp.get()
                    for kc in range(8):
                        mm(sc, ps, ps[:, 0:TM], wo, wo[:, kc, oc * 128:(oc + 1) * 128], mixT, mixT[:, kc, :], kc == 0, kc == 7)
                    sc.op("dve", lambda e: e.tensor_tensor(out=xt[:, oc, cs], in0=ps[:, 0:TM], in1=xt[:, oc, cs], op=ALU.add),
                          reads=[ps, xt], writes=[xt])

        with Scope(sc):
            if stage >= 3:
                emit_mlp(sc, cx, xts, wmi_d, wmo_d, "gmlp0")

        with Scope(sc):
            for i, xt in enumerate(xts):
                for c in range(8):
                    sc.dma("pool", "xo", x1T_d.t[c * 128:(c + 1) * 128, i * 512:(i + 1) * 512], xt[:, c, :],
                           reads=[xt], writes=[x1T_d])
            if stage >= 4:
                emit_kv(sc, cx, xts, wkv_d, kT_d, vtok_d)
            sc.finish([x1T_d, kT_d, vtok_d])
    return nc


def emit_kv(sc, cx, xts, wkv_d, kT_d, vtok_d):
    V = cx.vecs
    vo = VOFF
    ones, bones = cx.ones, cx.bones
    wkv = sc.sb((128, 8, 768), BF16, "wkv")
    load_weight_bf16(sc, cx, wkv_d, 1024, 0, 768, wkv, lambda kc: wkv[:, kc, :],
                     gcol_fn=lambda kc: V[:, vo["gkv"] + kc: vo["gkv"] + kc + 1])
    xb = sc.sb((128, 8, 512), BF16, "kv_xb")
    sqa = sc.sb((128, 8, 512), BF16, "kv_sq")
    rstd = sc.sb((128, 512), F32, "kv_rstd")
    tmp = sc.sb((128, 512), F32, "kv_tmp")
    kf_p = SbPool(sc, 2, (128, 512), F32, "kv_kf")
    rh = sc.sb((128, 512), F32, "kv_rh")
    ko_p = SbPool(sc, 2, (128, 512), BF16, "kv_ko")
    vo_p = SbPool(sc, 2, (128, 128), BF16, "kv_vo")
    for tt, xT in enumerate(xts):
        ps = cx.pp.get()
        for c in range(8):
            sc.op("act", lambda e: e.activation(out=sqa[:, c, :], in_=xT[:, c, :], func=AF.Square), reads=[xT], writes=[sqa])
            mm(sc, ps, ps[:, :], cx.consts, ones, sqa, sqa[:, c, :], c == 0, c == 7)
            sc.op("pool", lambda e: e.tensor_copy(out=xb[:, c, :], in_=xT[:, c, :]), reads=[xT], writes=[xb])
        rstd_from_ms(sc, ps, ps[:, :], rstd, rstd[:, :], tmp, tmp[:, :], 1.0 / D, 1e-6)
        for oi, (fc, gname) in enumerate([(0, None), (1, None), (2, "gk_s"), (4, "gk_w")]):
            ps = cx.pp.get()
            for kc in range(8):
                mm(sc, ps, ps[:, :], wkv, wkv[:, kc, fc * 128:(fc + 1) * 128], xb, xb[:, kc, :], kc == 0, kc == 7)
            ko = ko_p.get()
            if gname is None:
                sc.op("dve", lambda e: e.tensor_tensor(out=ko[:, :], in0=ps[:, :], in1=rstd[:, :], op=ALU.mult),
                      reads=[ps, rstd], writes=[ko])
            else:
                kf = kf_p.get()
                sc.op("dve", lambda e: e.tensor_tensor(out=kf[:, :], in0=ps[:, :], in1=rstd[:, :], op=ALU.mult),
                      reads=[ps, rstd], writes=[kf])
                ps2 = cx.pp.get()
                sumsq_bc(sc, cx, kf, lambda c: kf[:, :], 1, 512, bones, ps2, ps2[:, :])
                rstd_from_ms(sc, ps2, ps2[:, :], rh, rh[:, :], tmp, tmp[:, :], 1.0 / 64, 1e-6)
                g = V[:, vo[gname]: vo[gname] + 1]
                sc.op("dve", lambda e: e.scalar_tensor_tensor(out=ko[:, :], in0=kf[:, :], scalar=g, in1=rh[:, :],
                                                              op0=ALU.mult, op1=ALU.mult),
                      reads=[kf, V, rh], writes=[ko])
            sc.dma("pool", "ko" + ko.name, kT_d.t[oi, :, tt * 512:(tt + 1) * 512], ko[:, :], reads=[ko], writes=[kT_d])
        for sb_ in range(4):
            ts_ = slice(sb_ * 128, (sb_ + 1) * 128)
            pst = cx.pp.get()
            for c in range(8):
                mm(sc, pst, pst[:, 0:1], sqa, sqa[:, c, ts_], cx.consts, ones[:, 0:1], c == 0, c == 7)
            r = cx.small.get()
            sc.op("act", lambda e: e.activation(out=r[:, 0:1], in_=pst[:, 0:1], func=AF.Sqrt, scale=1.0 / D,
                                                bias=V[:, vo["eps6"]: vo["eps6"] + 1]),
                  reads=[pst, V], writes=[r])
            sc.op("dve", lambda e: e.reciprocal(out=r[:, 1:2], in_=r[:, 0:1]), reads=[r], writes=[r])
            for oi, fc in enumerate([3, 5]):
                ps = cx.pp.get()
                for kc in range(8):
                    mm(sc, ps, ps[:, 0:128], xb, xb[:, kc, ts_], wkv, wkv[:, kc, fc * 128:(fc + 1) * 128], kc == 0, kc == 7)
                vt = vo_p.get()
                sc.op("dve", lambda e: e.tensor_scalar(out=vt[:, :], in0=ps[:, 0:128], scalar1=r[:, 1:2], scalar2=None, op0=ALU.mult),
                      reads=[ps, r], writes=[vt])
                t0 = tt * 512 + sb_ * 128
                sc.dma("pool", "vo" + vt.name, vtok_d.t[oi, t0:t0 + 128, :], vt[:, :], reads=[vt], writes=[vtok_d])


NSLOT = 16


def slot_block(core, s):
    return 8 * s + (core if s % 2 == 0 else 7 - core)


def emit_compress(sc, cx, kT_d, kind, w1_d, w2_d, peT_col, is_k, kcmpT, vcmp):
    V = cx.vecs
    vo = VOFF
    ones, bones = cx.ones, cx.bones
    w1 = sc.sb((128, 32, 256), BF16, "cw1")
    for half in range(2):
        for lg in range(8):
            st = cx.wstage.get()
            src = w1_d.t[lg * 256:(lg + 1) * 256, :].rearrange("(l d) h -> d l h", d=64)
            sc.dma("sp", "w" + st.name, st[half * 64:(half + 1) * 64, 0:1024].rearrange("d (l h) -> d l h", l=4), src,
                   reads=[w1_d], writes=[st])
            sc.op("pool", lambda e: e.tensor_copy(out=w1[half * 64:(half + 1) * 64, lg * 4:(lg + 1) * 4, :],
                                                  in_=st[half * 64:(half + 1) * 64, 0:1024].rearrange("d (l h) -> d l h", l=4)),
                  reads=[st], writes=[w1])
    w2f = sc.sb((128, 2, 64), F32, "cw2f")
    for hc in range(2):
        sc.dma("sp", "cw2", w2f[:, hc, :], w2_d.t[hc * 128:(hc + 1) * 128, :], reads=[w2_d], writes=[w2f])
    w2 = sc.sb((128, 2, 64), BF16, "cw2")
    sc.op("dve", lambda e: e.tensor_copy(out=w2[:, :, :], in_=w2f[:, :, :]), reads=[w2f], writes=[w2])
    w2pad = sc.sb((128, 2, 2, 128), BF16, "cw2pad")
    sc.op("dve", lambda e: e.memset(w2pad[:, :, :, :], 0.0), writes=[w2pad])
    for g in range(2):
        sc.op("dve", lambda e: e.tensor_copy(out=w2pad[:, g, :, g * 64:(g + 1) * 64], in_=w2[:, :, :]), reads=[w2], writes=[w2pad])
    b1 = sc.sb((128, 2), F32, "cb1")
    for hc in range(2):
        ps = cx.pp.get()
        for l in range(32):
            mm(sc, ps, ps[:, 0:1], w1, w1[0:64, l, hc * 128:(hc + 1) * 128], cx.peT, cx.peT[0:64, peT_col, l:l + 1], l == 0, l == 31)
        sc.op("act", lambda e: e.activation(out=b1[:, hc:hc + 1], in_=ps[:, 0:1], func=AF.Copy), reads=[ps], writes=[b1])
    src_p = SbPool(sc, 1, (128, 8208), BF16, "csrc")
    hb = sc.sb((128, 512), F32, "chb")
    h2 = sc.sb((128, 512), F32, "ch2")
    gl = [[sc.sb((128, 512), BF16, f"cgl{g}{hc}") for hc in range(2)] for g in range(2)]
    kf = sc.sb((128, 512), F32, "ckf")
    rh = sc.sb((128, 512), F32, "crh")
    tmp = sc.sb((128, 512), F32, "ctmp")
    for nt in range(2):
        src = src_p.get()
        for q4 in range(4):
            sc.dma("sp", "csrc", src[:, q4 * 2052:(q4 + 1) * 2052], kT_d.t[kind, :, nt * 8192 + q4 * 2052: nt * 8192 + (q4 + 1) * 2052],
                   reads=[kT_d], writes=[src])
        for g in range(2):
            for hc in range(2):
                ps = cx.pp.get()
                for l in range(32):
                    mm(sc, ps, ps[:, :], w1, w1[g * 64:(g + 1) * 64, l, hc * 128:(hc + 1) * 128],
                       src, src[g * 64:(g + 1) * 64, l:l + 8177:16], l == 0, l == 31)
                sc.op("act", lambda e: e.activation(out=hb[:, :], in_=ps[:, :], func=AF.Identity, bias=b1[:, hc:hc + 1]),
                      reads=[ps, b1], writes=[hb])
                sc.op("dve", lambda e: e.tensor_tensor(out=h2[:, :], in0=hb[:, :], in1=hb[:, :], op=ALU.mult), reads=[hb], writes=[h2])
                sc.op("dve", lambda e: e.tensor_scalar(out=h2[:, :], in0=h2[:, :], scalar1=0.044715, scalar2=1.0, op0=ALU.mult, op1=ALU.add),
                      reads=[h2], writes=[h2])
                sc.op("dve", lambda e: e.tensor_tensor(out=h2[:, :], in0=h2[:, :], in1=hb[:, :], op=ALU.mult), reads=[h2, hb], writes=[h2])
                sc.op("act", lambda e: e.activation(out=h2[:, :], in_=h2[:, :], func=AF.Sigmoid, scale=1.5957691216), reads=[h2], writes=[h2])
                sc.op("dve", lambda e: e.tensor_tensor(out=gl[g][hc][:, :], in0=h2[:, :], in1=hb[:, :], op=ALU.mult),
                      reads=[h2, hb], writes=[gl[g][hc]])
        if is_k:
            ps = cx.pp.get()
            i = 0
            for g in range(2):
                for hc in range(2):
                    mm(sc, ps, ps[:, :], w2pad, w2pad[:, g, hc, :], gl[g][hc], gl[g][hc][:, :], i == 0, i == 3)
                    i += 1
            sc.op("act", lambda e: e.activation(out=kf[:, :], in_=ps[:, :], func=AF.Copy), reads=[ps], writes=[kf])
            ps2 = cx.pp.get()
            sumsq_bc(sc, cx, kf, lambda c: kf[:, :], 1, 512, bones, ps2, ps2[:, :])
            rstd_from_ms(sc, ps2, ps2[:, :], rh, rh[:, :], tmp, tmp[:, :], 1.0 / 64, 1e-6)
            gcol = V[:, vo["gk_c"]: vo["gk_c"] + 1]
            sc.op("dve", lambda e: e.scalar_tensor_tensor(out=kcmpT[:, nt * 512:(nt + 1) * 512], in0=kf[:, :], scalar=gcol, in1=rh[:, :],
                                                          op0=ALU.mult, op1=ALU.mult),
                  reads=[kf, V, rh], writes=[kcmpT])
        else:
            for sub in range(4):
                for g in range(2):
                    ps = cx.pp.get()
                    for hc in range(2):
                        mm(sc, ps, ps[:, 0:64], gl[g][hc], gl[g][hc][:, sub * 128:(sub + 1) * 128], w2, w2[:, hc, :], hc == 0, hc == 1)
                    sc.op("act", lambda e: e.activation(out=vcmp[:, nt * 4 + sub, g, 0:64], in_=ps[:, 0:64], func=AF.Copy),
                          reads=[ps], writes=[vcmp])


def build_l1(stage=99):
    nc = bass.Bass("TRN2", target_bir_lowering=False)
    es = contextlib.ExitStack()
    with es:
        sc = Sched(nc, es)
        cx = Ctx()
        x1T_d = sc.dram("x1T", (D, TPC), F32, "ExternalInput")
        kT_d = sc.dram("kTall", (3, 128, S + 16), BF16, "ExternalInput")
        vs_d = sc.dram("vs_all", (S, 128), BF16, "ExternalInput")
        kw_d = sc.dram("kw_own", (NSLOT, 128, 640), BF16, "ExternalInput")
        vw_d = sc.dram("vw_own", (NSLOT, 640, 128), BF16, "ExternalInput")
        cmask_d = sc.dram("cmask", (NSLOT, 128, 2, 128), BF16, "ExternalInput")
        dmask_d = sc.dram("dmask", (NSLOT, 128, 8, 128), BF16, "ExternalInput")
        wmask_d = sc.dram("wmask", (NSLOT, 128, 5, 128), BF16, "ExternalInput")
        fb_d = sc.dram("fbias", (NSLOT, 128, 256), F32, "ExternalInput")
        cexp_d = sc.dram("cexp", (128, 8192), BF16, "ExternalInput")
        selmap_d = sc.dram("selmap", (128, 8, 256), BF16, "ExternalInput")
        memT_d = sc.dram("memT", (D, 256), F32, "ExternalInput")
        wmem_d = sc.dram("w_mem_kv", (D, 512), F32, "ExternalInput")
        bwin_d = sc.dram("b_w_in", (D, 1060), F32, "ExternalInput")
        wout_d = sc.dram("w_out", (D, D), F32, "ExternalInput")
        wmi_d = sc.dram("w_mlp_in", (D, 4096), F32, "ExternalInput")
        wmo_d = sc.dram("w_mlp_out", (4096, D), F32, "ExternalInput")
        w1k_d = sc.dram("cmp_w1_k", (2048, 256), F32, "ExternalInput")
        w2k_d = sc.dram("cmp_w2_k", (256, 64), F32, "ExternalInput")
        w1v_d = sc.dram("cmp_w1_v", (2048, 256), F32, "ExternalInput")
        w2v_d = sc.dram("cmp_w2_v", (256, 64), F32, "ExternalInput")
        peT_d = sc.dram("peT", (64, 2, 32), F32, "ExternalInput")
        bgate_d = sc.dram("bgate", (128, 36), F32, "ExternalInput")
        x2T_d = sc.dram("x2T", (D, TPC), F32, "ExternalOutput")
        outT_d = sc.dram("outT", (D, TPC), F32, "ExternalOutput")
        common_setup(sc, nc, cx, npp=4)
        cx.acc = PsumPool(sc, 2, (128, 512), F32, "acc")
        V = cx.vecs
        vo = VOFF
        ones, bones, ident = cx.ones, cx.bones, cx.ident

        with Scope(sc):
            mkT = sc.sb((128, 2, 256), BF16, "mkT")
            mvp = sc.sb((128, 2, 4, 65), BF16, "mvp")
            kcmpT = sc.sb((128, 1024), BF16, "kcmpT")
            vcmp = sc.sb((128, 8, 2, 65), BF16, "vcmp")
            bgate = sc.sb((128, 36), F32, "bgate_sb")
            sc.dma("sp", "bg", bgate[:, :], bgate_d.t[:, :], reads=[bgate_d], writes=[bgate])
            with Scope(sc):
                emit_mem_kv(sc, cx, memT_d, wmem_d, 1, mkT, mvp)
            with Scope(sc):
                peTf = sc.sb((64, 2, 32), F32, "peTf")
                sc.dma("sp", "pe", peTf[:, :, :], peT_d.t[:, :, :], reads=[peT_d], writes=[peTf])
                cx.peT = sc.sb((64, 2, 32), BF16, "peTb")
                sc.op("dve", lambda e: e.tensor_copy(out=cx.peT[:, :, :], in_=peTf[:, :, :]), reads=[peTf], writes=[cx.peT])
                sc.op("pool", lambda e: e.memset(vcmp[:, :, :, 64:65], 1.0), writes=[vcmp])
                with Scope(sc):
                    emit_compress(sc, cx, kT_d, 0, w1k_d, w2k_d, 0, True, kcmpT, vcmp)
                with Scope(sc):
                    emit_compress(sc, cx, kT_d, 1, w1v_d, w2v_d, 1, False, kcmpT, vcmp)

            KsT = sc.sb((128, S), BF16, "KsT")
            for q4 in range(8):
                sc.dma("sp", "ks", KsT[:, q4 * 2048:(q4 + 1) * 2048], kT_d.t[2, :, q4 * 2048:(q4 + 1) * 2048], reads=[kT_d], writes=[KsT])
            Vs = sc.sb((128, 128, 2, 65), BF16, "Vs")
            sc.op("pool", lambda e: e.memset(Vs[:, :, :, 64:65], 1.0), writes=[Vs])
            vsv = vs_d.t.rearrange("(kt p) (g d) -> p kt g d", p=128, g=2)
            for q4 in range(16):
                for g in range(2):
                    sc.dma("sp", "vs", Vs[:, q4 * 8:(q4 + 1) * 8, g, 0:64], vsv[:, q4 * 8:(q4 + 1) * 8, g, :], reads=[vs_d], writes=[Vs])
            cexp = sc.sb((128, 8192), BF16, "cexp")
            sc.dma("sp", "cexp", cexp[:, :], cexp_d.t[:, :], reads=[cexp_d], writes=[cexp])
            selmap = sc.sb((128, 8, 256), BF16, "selmap")
            sc.dma("sp", "selmap", selmap[:, :, :], selmap_d.t[:, :, :], reads=[selmap_d], writes=[selmap])
            wb = sc.sb((128, 8, 1060), BF16, "wb")
            for p in range(2):
                load_weight_bf16(sc, cx, bwin_d, 1024, p * 530, 530, wb, lambda kc: wb[:, kc, p * 530:(p + 1) * 530],
                                 gcol_fn=lambda kc: V[:, vo["gmix1"] + kc: vo["gmix1"] + kc + 1])
            wo = sc.sb((128, 8, 1024), BF16, "wo")
            load_weight_bf16(sc, cx, wout_d, 1024, 0, 1024, wo, lambda kc: wo[:, kc, :])

            eTc_p = SbPool(sc, 8, (128, 768), BF16, "eTc")
            eT_p = SbPool(sc, 3, (128, 384), BF16, "eT")
            cmask_p = SbPool(sc, 1, (128, 2, 3, 128), BF16, "cmask")
            dmask_p = SbPool(sc, 1, (128, 8, 3, 128), BF16, "dmask")
            wmask_p = SbPool(sc, 1, (128, 5, 3, 128), BF16, "wmask")
            fb_p = SbPool(sc, 2, (128, 256), F32, "fb")
            kw_p = SbPool(sc, 2, (128, 640), BF16, "kw")
            vw_p = SbPool(sc, 2, (128, 5, 2, 65), BF16, "vw")
            for b in vw_p.bufs:
                sc.op("pool", lambda e: e.memset(b[:, :, :, 64:65], 1.0), writes=[b])
            xq_p = SbPool(sc, 2, (128, 8, 128), F32, "xq")
            xb = sc.sb((128, 8, 128), BF16, "xb1")
            sqa = sc.sb((128, 8, 128), BF16, "sqa1")
            rstd = sc.sb((128, 128), F32, "rstd1")
            tmp = sc.sb((128, 128), F32, "tmp1")
            qf = sc.sb((128, 128), F32, "qf1")
            rh = sc.sb((128, 128), F32, "rh1")
            qT = sc.sb((128, 6, 128), BF16, "qT")
            qmT = sc.sb((128, 2, 128), BF16, "qmT1")
            gates = sc.sb((128, 36), F32, "gates")
            imp = sc.sb((128, 256), F32, "imp")
            score2 = sc.sb((128, 256), F32, "score2")
            m8 = sc.sb((128, 16), F32, "m8")
            negsel = sc.sb((128, 256), BF16, "negsel")
            negselT = [sc.sb((128, 3, 128), BF16, f"negselT{i}") for i in range(2)]
            mixf = sc.sb((128, 768), F32, "mixf")
            mixtok = sc.sb((128, 1024), BF16, "mixtok1")
            mixT = sc.sb((128, 8, 128), BF16, "mixT1")
            x2 = SbPool(sc, 1, (128, 8, 128), F32, "x2")

            def attend(g, tiles, acc):
                first = True
                nt_ = len(tiles)
                for ti, (kb, kap, vb, vap, extra) in enumerate(tiles):
                    for half in range(2):
                        ps = cx.pp.get()
                        mm(sc, ps, ps[:, 0:384], kb, kap, qT, qT[g * 64:(g + 1) * 64, 3 * half:3 * half + 3, :], True, len(extra) == 0)
                        for xi, (lb, lap, rb, rapf) in enumerate(extra):
                            mm(sc, ps, ps[:, 0:384], lb, lap, rb, rapf(half), False, xi == len(extra) - 1)
                        eT = eT_p.get()
                        sc.op("act", lambda e: e.activation(out=eT[:, :], in_=ps[:, 0:384], func=AF.Exp, scale=0.125), reads=[ps], writes=[eT])
                        for h in range(3):
                            hh = 3 * half + h
                            last = (ti == nt_ - 1) and half == 1 and h == 2
                            mm(sc, acc, acc[:, hh * 65:(hh + 1) * 65], eT, eT[:, h * 128:(h + 1) * 128], vb, vap, first, last, skip=True)
                            first = False

            def combine(g, acc, br):
                r = cx.small.get()
                accv = acc[:, 0:390].rearrange("p (h e) -> p h e", h=6)
                sc.op("dve", lambda e: e.tensor_scalar(out=r[:, 0:6], in0=accv[:, :, 64], scalar1=1e-30, scalar2=None, op0=ALU.max),
                      reads=[acc], writes=[r])
                sc.op("dve", lambda e: e.reciprocal(out=r[:, 6:12], in_=r[:, 0:6]), reads=[r], writes=[r])
                gv = gates[:, :].rearrange("p (h t) -> p h t", t=3)
                r2 = cx.small.get()
                sc.op("dve", lambda e: e.tensor_tensor(out=r2[:, 0:6], in0=r[:, 6:12], in1=gv[:, 6 * g:6 * g + 6, br], op=ALU.mult),
                      reads=[r, gates], writes=[r2])
                for h in range(6):
                    H = 6 * g + h
                    if br == 0:
                        sc.op("dve", lambda e: e.tensor_scalar(out=mixf[:, H * 64:(H + 1) * 64], in0=acc[:, h * 65:h * 65 + 64],
                                                               scalar1=r2[:, h:h + 1], scalar2=None, op0=ALU.mult),
                              reads=[acc, r2], writes=[mixf])
                    else:
                        sc.op("dve", lambda e: e.scalar_tensor_tensor(out=mixf[:, H * 64:(H + 1) * 64], in0=acc[:, h * 65:h * 65 + 64],
                                                                      scalar=r2[:, h:h + 1], in1=mixf[:, H * 64:(H + 1) * 64],
                                                                      op0=ALU.mult, op1=ALU.add),
                              reads=[acc, r2, mixf], writes=[mixf])
                return r

            nslots = NSLOT if stage >= 2 else (1 if stage >= 1 else 0)
            for s in range(nslots):
                xq = xq_p.get()
                for c in range(8):
                    sc.dma("sp", "xq" + xq.name, xq[:, c, :], x1T_d.t[c * 128:(c + 1) * 128, s * 128:(s + 1) * 128], reads=[x1T_d], writes=[xq])
                cmask = cmask_p.get()
                dmask = dmask_p.get()
                wmask = wmask_p.get()
                for h in range(3):
                    sc.dma("sp", "cm" + cmask.name, cmask[:, :, h, :], cmask_d.t[s, :, :, :], reads=[cmask_d], writes=[cmask])
                    sc.dma("sp", "dm" + dmask.name, dmask[:, :, h, :], dmask_d.t[s, :, :, :], reads=[dmask_d], writes=[dmask])
                    sc.dma("sp", "wm" + wmask.name, wmask[:, :, h, :], wmask_d.t[s, :, :, :], reads=[wmask_d], writes=[wmask])
                fb = fb_p.get()
                sc.dma("sp", "fb" + fb.name, fb[:, :], fb_d.t[s, :, :], reads=[fb_d], writes=[fb])
                kw = kw_p.get()
                sc.dma("sp", "kw" + kw.name, kw[:, :], kw_d.t[s, :, :], reads=[kw_d], writes=[kw])
                vw = vw_p.get()
                for g in range(2):
                    sc.dma("sp", "vw" + vw.name, vw[:, :, g, 0:64],
                           vw_d.t[s, :, :].rearrange("(kt p) (g d) -> p kt g d", p=128, g=2)[:, :, g, :],
                           reads=[vw_d], writes=[vw])
                ps = cx.pp.get()
                for c in range(8):
                    sc.op("act", lambda e: e.activation(out=sqa[:, c, :], in_=xq[:, c, :], func=AF.Square), reads=[xq], writes=[sqa])
                    mm(sc, ps, ps[:, 0:128], cx.consts, ones, sqa, sqa[:, c, :], c == 0, c == 7)
                    sc.op("pool", lambda e: e.tensor_copy(out=xb[:, c, :], in_=xq[:, c, :]), reads=[xq], writes=[xb])
                rstd_from_ms(sc, ps, ps[:, 0:128], rstd, rstd[:, :], tmp, tmp[:, :], 1.0 / D, 1e-6)
                pst = cx.pp.get()
                for c in range(8):
                    mm(sc, pst, pst[:, 0:1], sqa, sqa[:, c, :], cx.consts, ones[:, 0:1], c == 0, c == 7)
                rt = cx.small.get()
                sc.op("act", lambda e: e.activation(out=rt[:, 0:1], in_=pst[:, 0:1], func=AF.Sqrt, scale=1.0 / D,
                                                    bias=V[:, vo["eps6"]: vo["eps6"] + 1]), reads=[pst, V], writes=[rt])
                sc.op("dve", lambda e: e.reciprocal(out=rt[:, 1:2], in_=rt[:, 0:1]), reads=[rt], writes=[rt])
                psg = cx.pp.get()
                for kc in range(8):
                    mm(sc, psg, psg[:, 0:36], xb, xb[:, kc, :], wb, wb[:, kc, 1024:1060], kc == 0, kc == 7)
                sc.op("dve", lambda e: e.scalar_tensor_tensor(out=gates[:, :], in0=psg[:, 0:36], scalar=rt[:, 1:2], in1=bgate[:, :],
                                                              op0=ALU.mult, op1=ALU.add), reads=[psg, rt, bgate], writes=[gates])
                sc.op("act", lambda e: e.activation(out=gates[:, :], in_=gates[:, :], func=AF.Sigmoid), reads=[gates], writes=[gates])
                for j in range(8):
                    ps = cx.pp.get()
                    for kc in range(8):
                        mm(sc, ps, ps[:, 0:128], wb, wb[:, kc, j * 128:(j + 1) * 128], xb, xb[:, kc, :], kc == 0, kc == 7)
                    sc.op("dve", lambda e: e.tensor_tensor(out=qf[:, :], in0=ps[:, 0:128], in1=rstd[:, :], op=ALU.mult), reads=[ps, rstd], writes=[qf])
                    ps2 = cx.pp.get()
                    sumsq_bc(sc, cx, qf, lambda c: qf[:, :], 1, 128, bones, ps2, ps2[:, 0:128])
                    rstd_from_ms(sc, ps2, ps2[:, 0:128], rh, rh[:, :], tmp, tmp[:, :], 1.0 / 64, 1e-6)
                    gn = "gq_nsa" if j < 6 else "gq_mem1"
                    dstb, dst = (qT, qT[:, j, :]) if j < 6 else (qmT, qmT[:, j - 6, :])
                    sc.op("dve", lambda e: e.scalar_tensor_tensor(out=dst, in0=qf[:, :], scalar=V[:, vo[gn]: vo[gn] + 1], in1=rh[:, :],
                                                                  op0=ALU.mult, op1=ALU.mult), reads=[qf, V, rh], writes=[dstb])
                ncmp = s // 2 + 1
                for g in range(2):
                    gs = slice(g * 64, (g + 1) * 64)
                    eTs = []
                    for j in range(ncmp):
                        eTc = eTc_p.get()
                        for half in range(2):
                            ps = cx.pp.get()
                            jj = j - (s // 2 - 1)
                            masked = jj >= 0
                            mm(sc, ps, ps[:, 0:384], kcmpT, kcmpT[gs, j * 128:(j + 1) * 128], qT, qT[gs, 3 * half:3 * half + 3, :], True, not masked)
                            if masked:
                                mm(sc, ps, ps[:, 0:384], cx.consts, ident, cmask, cmask[:, jj, :, :], False, True)
                            sc.op("act", lambda e: e.activation(out=eTc[:, half * 384:(half + 1) * 384], in_=ps[:, 0:384], func=AF.Exp, scale=0.125),
                                  reads=[ps], writes=[eTc])
                        eTs.append(eTc)
                    acc = cx.acc.get()
                    for h in range(6):
                        for j in range(ncmp):
                            mm(sc, acc, acc[:, h * 65:(h + 1) * 65], eTs[j], eTs[j][:, h * 128:(h + 1) * 128], vcmp, vcmp[:, j, g, :], j == 0, j == ncmp - 1)
                    r = combine(g, acc, 0)
                    for h in range(6):
                        ps = cx.pp.get()
                        for j in range(ncmp):
                            mm(sc, ps, ps[:, 0:256], eTs[j], eTs[j][:, h * 128:(h + 1) * 128], selmap, selmap[:, j, :], j == 0, j == ncmp - 1)
                        if h == 0:
                            sc.op("dve", lambda e: e.scalar_tensor_tensor(out=imp[:, :], in0=ps[:, 0:256], scalar=r[:, 6:7], in1=fb[:, :],
                                                                          op0=ALU.mult, op1=ALU.add), reads=[ps, r, fb], writes=[imp])
                        else:
                            sc.op("dve", lambda e: e.scalar_tensor_tensor(out=imp[:, :], in0=ps[:, 0:256], scalar=r[:, 6 + h:7 + h], in1=imp[:, :],
                                                                          op0=ALU.mult, op1=ALU.add), reads=[ps, r, imp], writes=[imp])
                    sc.op("dve", lambda e: e.max(out=m8[:, 0:8], in_=imp[:, :]), reads=[imp], writes=[m8])
                    sc.op("dve", lambda e: e.match_replace(out=score2[:, :], in_to_replace=m8[:, 0:8], in_values=imp[:, :], imm_value=-1e30),
                          reads=[m8, imp], writes=[score2])
                    sc.op("dve", lambda e: e.max(out=m8[:, 8:16], in_=score2[:, :]), reads=[score2], writes=[m8])
                    sc.op("dve", lambda e: e.tensor_scalar(out=negsel[:, :], in0=imp[:, :], scalar1=m8[:, 15:16], scalar2=NEG,
                                                           op0=ALU.is_lt, op1=ALU.mult), reads=[imp, m8], writes=[negsel])
                    for jt in range(2):
                        pt = cx.ptr.get()
                        sc.op("pe", lambda e: e.transpose(pt[:, 0:128], negsel[:, jt * 128:(jt + 1) * 128], ident), reads=[negsel, cx.consts], writes=[pt])
                        for h in range(3):
                            eng = "act" if h == 1 else "dve"
                            if eng == "act":
                                sc.op("act", lambda e: e.activation(out=negselT[jt][:, h, :], in_=pt[:, 0:128], func=AF.Copy), reads=[pt], writes=[negselT[jt]])
                            else:
                                sc.op("dve", lambda e: e.tensor_copy(out=negselT[jt][:, h, :], in_=pt[:, 0:128]), reads=[pt], writes=[negselT[jt]])
                    tiles = []
                    for kt in range(8 * (s + 1)):
                        k2 = kt % 64
                        extra = [(cexp, cexp[:, 128 * k2:128 * k2 + 128], negselT[kt // 64], (lambda half, b=negselT[kt // 64]: b[:, :, :]))]
                        if kt >= 8 * s:
                            extra.append((cx.consts, ident, dmask, (lambda half, kk=kt - 8 * s: dmask[:, kk, :, :])))
                        tiles.append((KsT, KsT[gs, kt * 128:(kt + 1) * 128], Vs, Vs[:, kt, g, :], extra))
                    acc = cx.acc.get()
                    attend(g, tiles, acc)
                    combine(g, acc, 1)
                    tiles = []
                    for kt in range(5):
                        extra = [(cx.consts, ident, wmask, (lambda half, kk=kt: wmask[:, kk, :, :]))]
                        tiles.append((kw, kw[gs, kt * 128:(kt + 1) * 128], vw, vw[:, kt, g, :], extra))
                    acc = cx.acc.get()
                    attend(g, tiles, acc)
                    combine(g, acc, 2)
                sc.op("act", lambda e: e.activation(out=mixtok[:, 0:768], in_=mixf[:, :], func=AF.Copy), reads=[mixf], writes=[mixtok])
                emit_mem_attn(sc, cx, qmT, 0, mkT, mvp, mixtok, 768)
                for j in range(8):
                    emit_transpose_to(sc, cx, mixtok, mixtok[:, j * 128:(j + 1) * 128], mixT, mixT[:, j, :])
                xo = x2.get()
                for oc in range(8):
                    ps = cx.pp.get()
                    for kc in range(8):
                        mm(sc, ps, ps[:, 0:128], wo, wo[:, kc, oc * 128:(oc + 1) * 128], mixT, mixT[:, kc, :], kc == 0, kc == 7)
                    sc.op("dve", lambda e: e.tensor_tensor(out=xo[:, oc, :], in0=ps[:, 0:128], in1=xq[:, oc, :], op=ALU.add), reads=[ps, xq], writes=[xo])
                    sc.dma("pool", "x2" + xo.name, x2T_d.t[oc * 128:(oc + 1) * 128, s * 128:(s + 1) * 128], xo[:, oc, :], reads=[xo], writes=[x2T_d])

        with Scope(sc):
            xts = [sc.sb((128, 8, 512), F32, f"xt{i}") for i in range(TPC // 512)]
            for i, xt in enumerate(xts):
                for c in range(8):
                    sc.dma("sp", f"x2l{i}", xt[:, c, :], x2T_d.t[c * 128:(c + 1) * 128, i * 512:(i + 1) * 512], reads=[x2T_d], writes=[xt])
            if stage >= 3:
                emit_mlp(sc, cx, xts, wmi_d, wmo_d, "gmlp1")
            for i, xt in enumerate(xts):
                for c in range(8):
                    sc.dma("pool", "xo", outT_d.t[c * 128:(c + 1) * 128, i * 512:(i + 1) * 512], xt[:, c, :], reads=[xt], writes=[outT_d])
            sc.finish([outT_d, x2T_d])
    return nc


def l0_inputs(inp):
    x = np.asarray(inp["x"], np.float32)[0]
    xpad = np.concatenate([np.zeros((HALO, D), np.float32), x], axis=0)
    consts = make_consts()
    memT = np.ascontiguousarray(np.asarray(inp["mem"], np.float32)[0].T)
    maps = []
    for c in range(NCORES):
        xs = xpad[c * TPC: c * TPC + HALO + TPC]
        maps.append({
            "xT": np.ascontiguousarray(xs.T),
            "memT": memT,
            "a_w_in": np.ascontiguousarray(inp["a_w_in"][0], dtype=np.float32),
            "w_out": np.ascontiguousarray(inp["w_out"][0], dtype=np.float32),
            "w_mlp_in": np.ascontiguousarray(inp["w_mlp_in"][0], dtype=np.float32),
            "w_mlp_out": np.ascontiguousarray(inp["w_mlp_out"][0], dtype=np.float32),
            "w_mem_kv": np.ascontiguousarray(inp["w_mem_kv"][0], dtype=np.float32),
            "w_kv": np.ascontiguousarray(inp["w_kv"], dtype=np.float32),
            "consts": consts,
            "vecs": pack_vecs(inp, c),
        })
    return maps


def run_l0(inp):
    nc = build_l0()
    res = run_bass_kernel_spmd(nc, l0_inputs(inp), core_ids=list(range(NCORES)))
    return res.results


def l1_consts():
    p = np.arange(128)
    cexp = (p[:, None] == (np.arange(8192)[None, :] // 64)).astype(np.float32)
    n = np.arange(1024)
    cs = n[:, None] * 16
    ss = np.arange(256)[None, :] * 64
    ov = np.minimum(cs + 32, ss + 64) - np.maximum(cs, ss)
    selmap = (np.clip(ov, 0, None) / 16).astype(np.float32)
    selmap[1023] = 0.0
    selmap = selmap.reshape(8, 128, 256).transpose(1, 0, 2)
    return cexp.astype(ml_dtypes.bfloat16), np.ascontiguousarray(selmap).astype(ml_dtypes.bfloat16)


def l1_masks(core):
    q = np.arange(128)
    p = np.arange(128)
    cm = np.zeros((NSLOT, 128, 2, 128), np.float32)
    dm = np.zeros((NSLOT, 128, 8, 128), np.float32)
    wm = np.zeros((NSLOT, 128, 5, 128), np.float32)
    fb = np.zeros((NSLOT, 128, 256), np.float32)
    for s in range(NSLOT):
        c = slot_block(core, s)
        t = 128 * c + q
        for jj in range(2):
            j = s // 2 - 1 + jj
            if j < 0:
                continue
            nblk = 128 * j + p
            vis = (16 * nblk[:, None] + 31) <= t[None, :]
            cm[s, :, jj, :] = np.where(vis, 0.0, NEG)
        for kk in range(8):
            kt = 8 * s + kk
            key = 128 * kt + p
            vis = key[:, None] <= t[None, :]
            dm[s, :, kk, :] = np.where(vis, 0.0, NEG)
        for kk in range(5):
            key = 128 * (c - 4 + kk) + p
            vis = (key[:, None] <= t[None, :]) & (key[:, None] > t[None, :] - 512) & (key[:, None] >= 0)
            wm[s, :, kk, :] = np.where(vis, 0.0, NEG)
        tb = t // 64
        jb = np.arange(256)
        forced = (jb[None, :] == 0) | (jb[None, :] == tb[:, None]) | (jb[None, :] == tb[:, None] - 1)
        fb[s] = np.where(forced, 1000.0, 0.0)
    bf = ml_dtypes.bfloat16
    return cm.astype(bf), dm.astype(bf), wm.astype(bf), fb


def l1_inputs(inp, x1T_cores, kT_all, vtok_all):
    bf = ml_dtypes.bfloat16
    x1T = np.concatenate(x1T_cores, axis=1)
    kpad = np.zeros((3, 128, S + 16), bf)
    kpad[:, :, :S] = kT_all[0:3]
    kwpad = np.zeros((128, 512 + S), bf)
    kwpad[:, 512:] = kT_all[3]
    vwpad = np.zeros((512 + S, 128), bf)
    vwpad[512:] = vtok_all[1]
    consts = make_consts()
    cexp, selmap = l1_consts()
    memT = np.ascontiguousarray(np.asarray(inp["mem"], np.float32)[0].T)
    perm = []
    for j in range(6):
        perm += list(range(j * 64, (j + 1) * 64)) + list(range((6 + j) * 64, (7 + j) * 64))
    perm += list(range(768, 1060))
    bw = np.ascontiguousarray(np.asarray(inp["b_w_in"][0], np.float32)[:, perm])
    peT = np.ascontiguousarray(np.stack([np.asarray(inp["cmp_pe_k"], np.float32).T, np.asarray(inp["cmp_pe_v"], np.float32).T], axis=1))
    bgate = np.ascontiguousarray(np.broadcast_to(np.asarray(inp["b_gate_b"][0], np.float32)[None, :], (128, 36)))
    maps = []
    for core in range(NCORES):
        blocks = [slot_block(core, s) for s in range(NSLOT)]
        xo = np.concatenate([x1T[:, 128 * c:128 * (c + 1)] for c in blocks], axis=1)
        kw = np.stack([kwpad[:, 128 * c:128 * c + 640] for c in blocks], axis=0)
        vw = np.stack([vwpad[128 * c:128 * c + 640, :] for c in blocks], axis=0)
        cm, dm, wm, fb = l1_masks(core)
        maps.append({
            "x1T": np.ascontiguousarray(xo), "kTall": kpad, "vs_all": np.ascontiguousarray(vtok_all[0]),
            "kw_own": np.ascontiguousarray(kw), "vw_own": np.ascontiguousarray(vw),
            "cmask": cm, "dmask": dm, "wmask": wm, "fbias": fb, "cexp": cexp, "selmap": selmap,
            "memT": memT,
            "w_mem_kv": np.ascontiguousarray(inp["w_mem_kv"][1], dtype=np.float32),
            "b_w_in": bw,
            "w_out": np.ascontiguousarray(inp["w_out"][1], dtype=np.float32),
            "w_mlp_in": np.ascontiguousarray(inp["w_mlp_in"][1], dtype=np.float32),
            "w_mlp_out": np.ascontiguousarray(inp["w_mlp_out"][1], dtype=np.float32),
            "cmp_w1_k": np.ascontiguousarray(inp["cmp_w1_k"], dtype=np.float32),
            "cmp_w2_k": np.ascontiguousarray(inp["cmp_w2_k"], dtype=np.float32),
            "cmp_w1_v": np.ascontiguousarray(inp["cmp_w1_v"], dtype=np.float32),
            "cmp_w2_v": np.ascontiguousarray(inp["cmp_w2_v"], dtype=np.float32),
            "peT": peT, "bgate": bgate, "consts": consts, "vecs": pack_vecs(inp, core),
        })
    return maps


def l1_gather(results, key="outT"):
    out = np.zeros((S, D), np.float32)
    for core in range(NCORES):
        oT = np.asarray(results[core][key])
        for s in range(NSLOT):
            c = slot_block(core, s)
            out[128 * c:128 * (c + 1), :] = oT[:, 128 * s:128 * (s + 1)].T
    return out


def kernel(**inputs):
    inp = {k: np.asarray(v) for k, v in inputs.items()}
    r0 = run_l0(inp)
    x1T_cores = [np.asarray(r0[c]["x1T"]) for c in range(NCORES)]
    kT_all = np.concatenate([np.asarray(r0[c]["kT"]) for c in range(NCORES)], axis=2)
    vtok_all = np.concatenate([np.asarray(r0[c]["vtok"]) for c in range(NCORES)], axis=1)
    nc = build_l1()
    res = run_bass_kernel_spmd(nc, l1_inputs(inp, x1T_cores, kT_all, vtok_all), core_ids=list(range(NCORES)))
    out = l1_gather(res.results)
    return out[None].astype(np.float32)
```
